# Optimizing a Trainium2 kernel written in Bass

```python
import jax, jax.numpy as jnp
from jax import lax
import numpy as np

D_MODEL = 2048
BATCH = 1
SEQ = 16384
DEPTH = 1

HEAD_DIM = 128
N_Q_HEADS = D_MODEL // HEAD_DIM
N_KV_HEADS = N_Q_HEADS // 4
Q_PER_KV = N_Q_HEADS // N_KV_HEADS
ATTN_WIDTH = N_Q_HEADS * HEAD_DIM
KV_WIDTH = N_KV_HEADS * HEAD_DIM
AXIS_ROPE_DIM = HEAD_DIM // 2
ROPE_THETA = 10000.0
Q_BLOCK = 128
FOURIER_GROUPS = 8
FOURIER_GROUP_DIM = 128
FOURIER_WIDTH = FOURIER_GROUPS * FOURIER_GROUP_DIM
N_BRANCHES = 2
IN_WIDTH = ATTN_WIDTH + 2 * KV_WIDTH + FOURIER_WIDTH + N_BRANCHES * D_MODEL
SPLIT_POINTS = [ATTN_WIDTH, ATTN_WIDTH + KV_WIDTH, ATTN_WIDTH + 2 * KV_WIDTH,
                ATTN_WIDTH + 2 * KV_WIDTH + FOURIER_WIDTH]
D_FF = ((8 * D_MODEL // 3 + 255) // 256) * 256
MACARON_WEIGHT = 0.5
GRID_W = 64
EPS = 1e-6

kernel_name = "hybrid_gated_fnet_axial_gqa_macaron"


def _rms_norm(x, g):
    xf = x.astype(jnp.float32)
    y = xf * lax.rsqrt(jnp.mean(xf * xf, axis=-1, keepdims=True) + EPS)
    return (y * g.astype(jnp.float32)).astype(x.dtype)


def _swiglu(x, w_gate, w_up, w_down):
    return (jax.nn.silu(x @ w_gate) * (x @ w_up)) @ w_down


def _macaron_ffn(x, pre_g, w_gate, w_up, w_down, post_g):
    return x + MACARON_WEIGHT * _rms_norm(_swiglu(_rms_norm(x, pre_g), w_gate, w_up, w_down), post_g)


def _axial_rope_tables(seq):
    rows = seq // GRID_W
    row = jnp.repeat(jnp.arange(rows, dtype=jnp.float32), GRID_W)
    col = jnp.tile(jnp.arange(GRID_W, dtype=jnp.float32), rows)
    n_freq = AXIS_ROPE_DIM // 2
    inv_freq = ROPE_THETA ** (-jnp.arange(n_freq, dtype=jnp.float32) / n_freq)
    ang_r = row[:, None] * inv_freq
    ang_c = col[:, None] * inv_freq
    return (jnp.cos(ang_r), jnp.sin(ang_r), jnp.cos(ang_c), jnp.sin(ang_c))


def _rotate(xs, cos, sin):
    x1, x2 = jnp.split(xs.astype(jnp.float32), 2, axis=-1)
    c = cos[None, :, None, :]
    s = sin[None, :, None, :]
    return jnp.concatenate([x1 * c - x2 * s, x2 * c + x1 * s], axis=-1)


def _apply_axial_rope(x, tables):
    cos_r, sin_r, cos_c, sin_c = tables
    x_row, x_col = jnp.split(x, 2, axis=-1)
    return jnp.concatenate([_rotate(x_row, cos_r, sin_r), _rotate(x_col, cos_c, sin_c)], axis=-1).astype(x.dtype)


def _blocked_gqa(q, k, v):
    b, s = q.shape[0], q.shape[1]
    n_blk = s // Q_BLOCK
    qb = q.reshape(b, n_blk, Q_BLOCK, N_KV_HEADS, Q_PER_KV, HEAD_DIM).transpose(1, 0, 2, 3, 4, 5)
    scale = HEAD_DIM ** -0.5

    def attend(q_blk):
        sc = jnp.einsum('bqkgd,bskd->bkgqs', q_blk, k, preferred_element_type=jnp.float32) * scale
        p = jax.nn.softmax(sc, axis=-1)
        return jnp.einsum('bkgqs,bskd->bqkgd', p.astype(v.dtype), v)

    o = lax.map(attend, qb)
    return o.transpose(1, 0, 2, 3, 4, 5).reshape(b, s, ATTN_WIDTH)


def _fourier_mix(f):
    b, s = f.shape[0], f.shape[1]
    fg = f.astype(jnp.float32).reshape(b, s, FOURIER_GROUPS, FOURIER_GROUP_DIM)
    mixed = jnp.real(jnp.fft.fft2(fg, axes=(1, 3), norm='ortho'))
    return mixed.reshape(b, s, FOURIER_WIDTH).astype(f.dtype)


def _gated_mixers(h, pre_g, w_in, b_gate, q_norm_g, k_norm_g, w_attn_o, w_fourier, w_out, post_g, rope):
    b, s, _ = h.shape
    u = _rms_norm(h, pre_g)
    q, k, v, f, g = jnp.split(u @ w_in, SPLIT_POINTS, axis=-1)
    q = _apply_axial_rope(_rms_norm(q.reshape(b, s, N_Q_HEADS, HEAD_DIM), q_norm_g), rope)
    k = _apply_axial_rope(_rms_norm(k.reshape(b, s, N_KV_HEADS, HEAD_DIM), k_norm_g), rope)
    v = v.reshape(b, s, N_KV_HEADS, HEAD_DIM)
    y_attn = _blocked_gqa(q, k, v) @ w_attn_o
    y_four = _fourier_mix(f) @ w_fourier
    gates = jax.nn.sigmoid((g.reshape(b, s, N_BRANCHES, D_MODEL) + b_gate).astype(jnp.float32)).astype(h.dtype)
    merged = gates[:, :, 0] * y_attn + gates[:, :, 1] * y_four
    return h + _rms_norm(merged @ w_out, post_g)


def setup_inputs(seed: int = 0) -> dict:
    key = jax.random.key(seed)
    ks = jax.random.split(key, 21)

    def w(k, shape, fan_in):
        return jax.random.normal(k, shape, jnp.float32) * (fan_in ** -0.5)

    def gain(k, shape):
        return 1.0 + 0.05 * jax.random.normal(k, shape, jnp.float32)

    L = DEPTH
    return {
        "x": jax.random.normal(ks[0], (BATCH, SEQ, D_MODEL), jnp.float32),
        "ffn1_pre_g": gain(ks[1], (L, D_MODEL)),
        "ffn1_w_gate": w(ks[2], (L, D_MODEL, D_FF), D_MODEL),
        "ffn1_w_up": w(ks[3], (L, D_MODEL, D_FF), D_MODEL),
        "ffn1_w_down": w(ks[4], (L, D_FF, D_MODEL), D_FF),
        "ffn1_post_g": gain(ks[5], (L, D_MODEL)),
        "mix_pre_g": gain(ks[6], (L, D_MODEL)),
        "w_in": w(ks[7], (L, D_MODEL, IN_WIDTH), D_MODEL),
        "b_gate": 0.1 * jax.random.normal(ks[8], (L, N_BRANCHES, D_MODEL), jnp.float32),
        "q_norm_g": gain(ks[9], (L, HEAD_DIM)),
        "k_norm_g": gain(ks[10], (L, HEAD_DIM)),
        "w_attn_o": w(ks[11], (L, ATTN_WIDTH, D_MODEL), ATTN_WIDTH),
        "w_fourier": w(ks[12], (L, FOURIER_WIDTH, D_MODEL), FOURIER_WIDTH),
        "w_out": w(ks[13], (L, D_MODEL, D_MODEL), D_MODEL),
        "mix_post_g": gain(ks[14], (L, D_MODEL)),
        "ffn2_pre_g": gain(ks[15], (L, D_MODEL)),
        "ffn2_w_gate": w(ks[16], (L, D_MODEL, D_FF), D_MODEL),
        "ffn2_w_up": w(ks[17], (L, D_MODEL, D_FF), D_MODEL),
        "ffn2_w_down": w(ks[18], (L, D_FF, D_MODEL), D_FF),
        "ffn2_post_g": gain(ks[19], (L, D_MODEL)),
    }


def reference(x, ffn1_pre_g, ffn1_w_gate, ffn1_w_up, ffn1_w_down, ffn1_post_g,
              mix_pre_g, w_in, b_gate, q_norm_g, k_norm_g, w_attn_o, w_fourier, w_out, mix_post_g,
              ffn2_pre_g, ffn2_w_gate, ffn2_w_up, ffn2_w_down, ffn2_post_g):
    rope = _axial_rope_tables(x.shape[1])
    h = x
    for l in range(DEPTH):
        h = _macaron_ffn(h, ffn1_pre_g[l], ffn1_w_gate[l], ffn1_w_up[l], ffn1_w_down[l], ffn1_post_g[l])
        h = _gated_mixers(h, mix_pre_g[l], w_in[l], b_gate[l], q_norm_g[l], k_norm_g[l],
                          w_attn_o[l], w_fourier[l], w_out[l], mix_post_g[l], rope)
        h = _macaron_ffn(h, ffn2_pre_g[l], ffn2_w_gate[l], ffn2_w_up[l], ffn2_w_down[l], ffn2_post_g[l])
    return h
```

```python
import os
import numpy as np
import ml_dtypes
import concourse.bass as bass
import concourse.mybir as mybir
from concourse.bass_utils import run_bass_kernel_spmd

F32 = mybir.dt.float32
BF16 = mybir.dt.bfloat16
AF = mybir.ActivationFunctionType
ALU = mybir.AluOpType

NCORES = 8
STOP = os.environ.get("KSTOP", "")
TB = 512
EPS = 1e-6
FULL_CFG = dict(D=2048, FF=5632, SEQ=16384)


class Tok:
    __slots__ = ("sem", "val")

    def __init__(self, sem, val):
        self.sem = sem
        self.val = val


class Buf:
    __slots__ = ("name", "w", "r")

    def __init__(self, name):
        self.name = name
        self.w = None
        self.r = {}


class Stream:
    def __init__(self, name, sem=None, dma_sems=None):
        self.name = name
        self.sem = sem
        self.count = 0
        self.seen = {}
        self.ops = []
        self.dma_sems = dma_sems or []
        self.dma_cnt = [0] * len(self.dma_sems)
        self.dma_rr = 0


class Plan:
    def __init__(self):
        self.streams = {}
        self.extra_toks = []

    def add(self, st):
        self.streams[st.name] = st
        return st

    def _wait(self, st, t):
        k = id(t.sem)
        if st.seen.get(k, 0) < t.val:
            st.seen[k] = t.val
            st.ops.append(("wait", t.sem, t.val))

    def _deps(self, st, reads, writes):
        toks = []
        for b in reads:
            if b.w is not None:
                toks.append(b.w)
        for b in writes:
            if b.w is not None:
                toks.append(b.w)
            toks.extend(b.r.values())
        for t in toks:
            if st.name == "pe" and t.sem is st.sem:
                continue
            self._wait(st, t)

    def _mark(self, tok, reads, writes):
        k = id(tok.sem)
        for b in reads:
            o = b.r.get(k)
            if o is None or o.val < tok.val:
                b.r[k] = tok
        for b in writes:
            b.w = tok
            b.r = {}

    def op(self, st, fn, reads=(), writes=()):
        self._deps(st, reads, writes)
        st.count += 1
        tok = Tok(st.sem, st.count)
        st.ops.append(("op", fn, st.sem, 1))
        self._mark(tok, reads, writes)
        return tok

    def group(self, st, fns, reads=(), writes=()):
        self._deps(st, reads, writes)
        for fn in fns[:-1]:
            st.ops.append(("op", fn, None, 0))
        st.count += 1
        tok = Tok(st.sem, st.count)
        st.ops.append(("op", fns[-1], st.sem, 1))
        self._mark(tok, reads, writes)
        return tok

    def dma(self, st, fn, reads=(), writes=()):
        i = st.dma_rr
        st.dma_rr = (i + 1) % len(st.dma_sems)
        sem = st.dma_sems[i]
        if st.dma_cnt[i] > 0:
            self._wait(st, Tok(sem, 16 * st.dma_cnt[i]))
        self._deps(st, reads, writes)
        st.dma_cnt[i] += 1
        tok = Tok(sem, 16 * st.dma_cnt[i])
        st.ops.append(("op", fn, sem, 16))
        self._mark(tok, reads, writes)
        return tok

    def coll(self, st, fn, sem, reads=(), writes=()):
        self._deps(st, reads, writes)
        tok = Tok(sem, 1)
        st.ops.append(("op", fn, sem, -1))
        self._mark(tok, reads, writes)
        self.extra_toks.append(tok)
        return tok

    def all_tokens(self):
        toks = []
        for st in self.streams.values():
            if st.sem is not None and st.count > 0:
                toks.append(Tok(st.sem, st.count))
            for sem, c in zip(st.dma_sems, st.dma_cnt):
                if c > 0:
                    toks.append(Tok(sem, 16 * c))
        return toks

    def barrier(self, only=None):
        toks = self.all_tokens()
        for st in self.streams.values():
            if only is not None and st.name not in only:
                continue
            for t in toks:
                self._wait(st, t)

    def emit(self, st, eng):
        for o in st.ops:
            if o[0] == "wait":
                eng.wait_ge(o[1], o[2])
            else:
                ins = o[1](eng)
                if o[2] is not None:
                    if o[3] == -1:
                        ins.then_inc(o[2])
                    else:
                        ins.then_inc(o[2], o[3])
        st.ops = []


class Ring:
    def __init__(self, items):
        self.items = list(items)
        self.i = 0

    def next(self):
        v = self.items[self.i]
        self.i = (self.i + 1) % len(self.items)
        return v


def build_program(cfg):
    D, FF, SEQ = cfg["D"], cfg["FF"], cfg["SEQ"]
    DC = D // 128
    FC = FF // 128
    NH = D // 128
    NKV = NH // 4
    QW = D
    KVW = NKV * 128
    FW = 1024
    INW = QW + 2 * KVW + FW + 2 * D
    TPC = SEQ // NCORES
    NB = TPC // TB
    TBL = TPC // 128
    S1 = SEQ // 128
    KC = SEQ // 128
    CPB = max(1, 512 // (2 * S1))
    assert S1 <= 128 and DC % 4 == 0 and FC % 4 == 0 and FC >= 2 * DC + 8 and QW % 512 == 0
    KMAX = max(DC, NH, 8)
    SCALE = 128.0 ** -0.5
    NV = 8 * DC + 4
    NCB = 2 * S1 + 256 + 256 + 128 + 128 + 128 + 128

    nc = bass.Bass("TRN2", target_bir_lowering=False)

    def din(name, shape, dt=F32):
        return nc.dram_tensor(name, list(shape), dt, kind="ExternalInput").ap()

    def dint(name, shape, dt):
        return nc.dram_tensor(name, list(shape), dt, kind="Internal").ap()

    x = din("x", [TPC, D])
    w = {}
    for f in ("ffn1", "ffn2"):
        w[f + "_w_gate"] = din(f + "_w_gate", [D, FF])
        w[f + "_w_up"] = din(f + "_w_up", [D, FF])
        w[f + "_w_down"] = din(f + "_w_down", [FF, D])
    w_in = din("w_in", [D, INW])
    w_attn_o = din("w_attn_o", [QW, D])
    w_fourier = din("w_fourier", [FW, D])
    w_out = din("w_out", [D, D])
    vecs_d = din("vecs", [128, NV])
    ident_d = din("ident", [128, 128])
    ropeT_d = din("ropeT", [128, 128])
    tw_d = din("tw", [128, 2, CPB, S1])
    ropetab_d = din("ropetab", [128, 2, TPC])
    cb16_d = din("cb16", [128, NCB], BF16)
    y = nc.dram_tensor("y", [TPC, D], F32, kind="ExternalOutput").ap()

    h_sp = dint("h_sp", [D, TPC], F32)
    q_sp = dint("q_sp", [QW, TPC], BF16)
    kT_src = dint("kT_src", [KVW, TPC], BF16)
    kT_all = dint("kT_all", [NCORES * KVW, TPC], BF16)
    v_src = dint("v_src", [KVW, TBL * 128], BF16)
    v_all = dint("v_all", [NCORES * KVW, TBL * 128], BF16)
    f_src = dint("f_src", [8 * TBL, 128 * 128], BF16)
    f_all = dint("f_all", [NCORES * 8 * TBL, 128 * 128], BF16)
    mix_src = dint("mix_src", [128, SEQ], BF16)
    mix_all = dint("mix_all", [NCORES * 128, SEQ], BF16)

    RG = [list(range(NCORES))]
    plan = Plan()

    from contextlib import ExitStack
    with ExitStack() as es:
        def sem(name):
            return es.enter_context(nc.semaphore(name))

        pe = plan.add(Stream("pe", sem("s_pe")))
        act = plan.add(Stream("act", sem("s_act")))
        dve = plan.add(Stream("dve", sem("s_dve")))
        pool = plan.add(Stream("pool", sem("s_pool"), [sem("dq%d" % i) for i in range(12)]))
        sp = plan.add(Stream("sp", sem("s_sp"), [sem("ds%d" % i) for i in range(24)]))
        cc_sems = [sem("cc%d" % i) for i in range(4)]

        ps = es.enter_context(nc.psum_tensor("ps", [128, 8, 512], F32))
        PB = [Buf("pb%d" % i) for i in range(8)]

        vecs = es.enter_context(nc.sbuf_tensor("vecs_sb", [128, NV], F32))
        ident = es.enter_context(nc.sbuf_tensor("ident_sb", [128, 128], F32))
        ropeT = es.enter_context(nc.sbuf_tensor("ropeT_sb", [128, 128], F32))
        cb16 = es.enter_context(nc.sbuf_tensor("cb16_sb", [128, NCB], BF16))
        CONST = Buf("const")
        o = 0
        D1 = cb16[0:S1, o:o + 2 * S1]; o += 2 * S1
        D2a = cb16[:, o:o + 256]; o += 256
        D2b = cb16[:, o:o + 256]; o += 256
        Cc = cb16[:, o:o + 128]; o += 128
        Sc = cb16[:, o:o + 128]; o += 128
        ones_bf = cb16[:, o:o + 128]; o += 128
        ropeT_bf = cb16[:, o:o + 128]; o += 128

        def vcol(i):
            return vecs[:, i:i + 1]

        EPSC = 8 * DC + 2
        QG = 8 * DC
        KG = 8 * DC + 1

        pid_cache = {}

        def get_pid(e, key):
            k = (key, id(e))
            if k not in pid_cache:
                pid_cache[k] = e.partition_id()
            return pid_cache[k]

        def run_block(phase_key):
            with nc.Block() as block:
                @block.tensor
                def _(e):
                    plan.emit(pe, e)

                @block.scalar
                def _(e):
                    plan.emit(act, e)

                @block.vector
                def _(e):
                    plan.emit(dve, e)

                @block.gpsimd
                def _(e):
                    plan.emit(pool, e)

                @block.sync
                def _(e):
                    plan.emit(sp, e)

        plan.dma(sp, lambda e: e.dma_start(out=vecs[:], in_=vecs_d), writes=[CONST])
        plan.dma(sp, lambda e: e.dma_start(out=ident[:], in_=ident_d), writes=[CONST])
        plan.dma(sp, lambda e: e.dma_start(out=ropeT[:], in_=ropeT_d), writes=[CONST])
        plan.dma(sp, lambda e: e.dma_start(out=cb16[:], in_=cb16_d), writes=[CONST])
        plan.barrier()

        def token_phase(which):
            with ExitStack() as ps_es:
                def sb(name, shape, dt):
                    return ps_es.enter_context(nc.sbuf_tensor(name + which, list(shape), dt))

                T32 = sb("T32", [128, DC, TB], F32)
                uT = sb("uT", [128, DC, TB], BF16)
                actT = sb("actT", [128, FC, TB], BF16)
                wgu = sb("wgu", [128, 4, KMAX, 512], BF16)
                wd = sb("wd", [128, 3, 4, 512], BF16)
                xin = sb("xin", [128, 2, 1024], F32)
                sqs = sb("sqs", [128, 2, TB], BF16)
                sil = sb("sil", [128, 2, TB], BF16)
                rstd = sb("rstd", [128, TB], F32)
                tq = sb("tq", [128, TB], F32)
                tb_ = sb("tb", [128, TB], F32)
                rs = sb("rs", [128, TB], F32)
                hl = sb("hl", [128, 2, TB], BF16)
                HL = Buf("hl")
                st16 = sb("st16", [128, 4, TB], BF16)
                yst = sb("yst", [128, 2, 512], F32)
                rtab = sb("rtab", [128, 2, TB], F32)

                T32B = [Buf("t32_%d" % c) for c in range(DC)]
                UT = [Buf("ut%d" % c) for c in range(DC)]
                ACTB = [Buf("act%d" % c) for c in range(FC)]
                WS = [Buf("ws%d" % i) for i in range(4)]
                WD = [Buf("wd%d" % i) for i in range(3)]
                XIN = [Buf("xin%d" % i) for i in range(2)]
                SQ = [Buf("sq%d" % i) for i in range(2)]
                SIL = [Buf("sil%d" % i) for i in range(2)]
                RSTD, TQ, TBB, RS, RTAB = Buf("rstd"), Buf("tq"), Buf("tb"), Buf("rs"), Buf("rtab")
                ST16 = [Buf("st16_%d" % i) for i in range(4)]
                YST = [Buf("yst%d" % i) for i in range(2)]
                HSP = [Buf("hsp%d" % b) for b in range(NB)]
                rWS, rWD, rXIN, rSQ, rSIL, rST, rYST = (Ring(range(4)), Ring(range(3)), Ring(range(2)), Ring(range(2)),
                                                         Ring(range(2)), Ring(range(4)), Ring(range(2)))
                ringA = Ring([0, 1, 2, 3])
                ringB = Ring([4, 5, 6, 7])
                ringAll = Ring(range(8))
                alt = [0]

                def evac_copy(out_ap, in_ap, reads, writes, scale=None):
                    alt[0] ^= 1
                    if alt[0] or scale is not None:
                        if scale is None:
                            return plan.op(act, lambda e: e.activation(out=out_ap, in_=in_ap, func=AF.Copy), reads, writes)
                        return plan.op(act, lambda e: e.activation(out=out_ap, in_=in_ap, func=AF.Copy, scale=scale), reads, writes)
                    return plan.op(dve, lambda e: e.tensor_copy(out=out_ap, in_=in_ap), reads, writes)

                def load_w(W, col0, ncols, kc):
                    s = rWS.next()
                    src = W[:, col0:col0 + ncols].rearrange("(c p) f -> p c f", p=128)
                    plan.dma(pool, lambda e: e.dma_start(out=wgu[:, s, 0:kc, 0:ncols], in_=src), writes=[WS[s]])
                    return s

                def stats(chunks, nfeat, out_t, out_b, ring):
                    bank = ring.next()
                    n = len(chunks)
                    for i, (ap, buf) in enumerate(chunks):
                        s = rSQ.next()
                        plan.op(act, lambda e, ap=ap, s=s: e.activation(out=sqs[:, s, :], in_=ap, func=AF.Square),
                                reads=[buf], writes=[SQ[s]])
                        plan.group(pe, [lambda e, s=s, i=i: e.matmul(ps[:, bank, :], lhsT=ones_bf, rhs=sqs[:, s, :],
                                                                     start=(i == 0), stop=(i == n - 1))],
                                   reads=[SQ[s]], writes=[PB[bank]])
                    plan.op(act, lambda e: e.activation(out=out_t[:], in_=ps[:, bank, :], func=AF.Sqrt,
                                                        bias=vcol(EPSC), scale=1.0 / nfeat),
                            reads=[PB[bank]], writes=[out_b])
                    plan.op(dve, lambda e: e.reciprocal(out=out_t[:], in_=out_t[:]), reads=[out_b], writes=[out_b])

                def prenorm(gi):
                    stats([(T32[:, c, :], T32B[c]) for c in range(DC)], D, rstd, RSTD, ringB)
                    for c in range(DC):
                        plan.op(dve, lambda e, c=c: e.scalar_tensor_tensor(
                            out=uT[:, c, :], in0=T32[:, c, :], scalar=vcol(gi * DC + c), in1=rstd[:],
                            op0=ALU.mult, op1=ALU.mult), reads=[T32B[c], RSTD], writes=[UT[c]])

                def hsp_view(t0):
                    return h_sp.rearrange("(c p) t -> p c t", p=128)[:, :, t0:t0 + TB]

                def spill_T32(b, t0):
                    plan.dma(sp, lambda e: e.dma_start(out=hsp_view(t0), in_=T32[:]), reads=T32B, writes=[HSP[b]])

                def postnorm_residual(gi, factor, b, t0):
                    stats([(T32[:, c, :], T32B[c]) for c in range(DC)], D, rstd, RSTD, ringB)
                    for c in range(DC):
                        plan.op(dve, lambda e, c=c: e.scalar_tensor_tensor(
                            out=T32[:, c, :], in0=T32[:, c, :], scalar=vcol(gi * DC + c), in1=rstd[:],
                            op0=ALU.mult, op1=ALU.mult), reads=[T32B[c], RSTD], writes=[T32B[c]])
                    hv = hsp_view(t0)
                    for cq in range(DC // 2):
                        s = rXIN.next()
                        xv = xin[:, s, :].rearrange("p (c t) -> p c t", c=2)
                        plan.dma(sp, lambda e, cq=cq, xv=xv: e.dma_start(out=xv, in_=hv[:, 2 * cq:2 * cq + 2, :]),
                                 reads=[HSP[b]], writes=[XIN[s]])
                        plan.op(dve, lambda e, cq=cq, xv=xv: e.scalar_tensor_tensor(
                            out=T32[:, 2 * cq:2 * cq + 2, :], in0=T32[:, 2 * cq:2 * cq + 2, :], scalar=float(factor),
                            in1=xv, op0=ALU.mult, op1=ALU.add),
                            reads=[T32B[2 * cq], T32B[2 * cq + 1], XIN[s]], writes=[T32B[2 * cq], T32B[2 * cq + 1]])

                def ffn(pfx, gi_pre, gi_post, b, t0):
                    Wg, Wu, Wd = w[pfx + "_w_gate"], w[pfx + "_w_up"], w[pfx + "_w_down"]
                    prenorm(gi_pre)
                    spill_T32(b, t0)
                    for fg in range(FF // 512):
                        sg = load_w(Wg, fg * 512, 512, DC)
                        su = load_w(Wu, fg * 512, 512, DC)
                        for j in range(4):
                            fc = fg * 4 + j
                            bg = ringA.next()
                            bu = ringA.next()
                            plan.group(pe, [lambda e, c=c, j=j, sg=sg, bg=bg: e.matmul(
                                ps[:, bg, :], lhsT=wgu[:, sg, c, j * 128:(j + 1) * 128], rhs=uT[:, c, :],
                                start=(c == 0), stop=(c == DC - 1)) for c in range(DC)],
                                reads=[WS[sg]] + UT, writes=[PB[bg]])
                            plan.group(pe, [lambda e, c=c, j=j, su=su, bu=bu: e.matmul(
                                ps[:, bu, :], lhsT=wgu[:, su, c, j * 128:(j + 1) * 128], rhs=uT[:, c, :],
                                start=(c == 0), stop=(c == DC - 1)) for c in range(DC)],
                                reads=[WS[su]] + UT, writes=[PB[bu]])
                            s = rSIL.next()
                            plan.op(act, lambda e, s=s, bg=bg: e.activation(out=sil[:, s, :], in_=ps[:, bg, :], func=AF.Silu),
                                    reads=[PB[bg]], writes=[SIL[s]])
                            plan.op(dve, lambda e, s=s, bu=bu, fc=fc: e.tensor_tensor(
                                out=actT[:, fc, :], in0=sil[:, s, :], in1=ps[:, bu, :], op=ALU.mult),
                                reads=[SIL[s], PB[bu]], writes=[ACTB[fc]])
                    for qd in range(DC // 4):
                        banks = [ringB.next() for _ in range(4)]
                        for fcg in range(FC // 4):
                            s = rWD.next()
                            src = Wd[fcg * 512:(fcg + 1) * 512, qd * 512:(qd + 1) * 512].rearrange("(f p) n -> p f n", p=128)
                            plan.dma(pool, lambda e, s=s, src=src: e.dma_start(out=wd[:, s, :, :], in_=src), writes=[WD[s]])
                            for fi in range(4):
                                fc = fcg * 4 + fi
                                plan.group(pe, [lambda e, di=di, fi=fi, fc=fc, s=s, banks=banks: e.matmul(
                                    ps[:, banks[di], :], lhsT=wd[:, s, fi, di * 128:(di + 1) * 128], rhs=actT[:, fc, :],
                                    start=(fc == 0), stop=(fc == FC - 1)) for di in range(4)],
                                    reads=[WD[s], ACTB[fc]], writes=[PB[bk] for bk in banks])
                        for di in range(4):
                            c = qd * 4 + di
                            evac_copy(T32[:, c, :], ps[:, banks[di], :], [PB[banks[di]]], [T32B[c]])
                    postnorm_residual(gi_post, 0.5, b, t0)

                def proj_fm(W, col0, ncols, kc, rhs_of, rhs_bufs, handler, ring):
                    s = load_w(W, col0, ncols, kc)
                    for j in range(ncols // 128):
                        bank = ring.next()
                        plan.group(pe, [lambda e, c=c, j=j, s=s, bank=bank: e.matmul(
                            ps[:, bank, :], lhsT=wgu[:, s, c, j * 128:(j + 1) * 128], rhs=rhs_of(c),
                            start=(c == 0), stop=(c == kc - 1)) for c in range(kc)],
                            reads=[WS[s]] + rhs_bufs, writes=[PB[bank]])
                        handler(j, bank)

                def head_post(bank, gcol, dst_ap, dst_buf):
                    s = rSQ.next()
                    plan.op(act, lambda e: e.activation(out=sqs[:, s, :], in_=ps[:, bank, :], func=AF.Square),
                            reads=[PB[bank]], writes=[SQ[s]])
                    plan.op(act, lambda e: e.activation(out=tq[:], in_=ps[:, bank, :], func=AF.Copy, scale=vcol(gcol)),
                            reads=[PB[bank]], writes=[TQ])
                    b2 = ringB.next()
                    plan.group(pe, [lambda e: e.matmul(ps[:, b2, :], lhsT=ones_bf, rhs=sqs[:, s, :], start=True, stop=True)],
                               reads=[SQ[s]], writes=[PB[b2]])
                    b3 = ringB.next()
                    plan.op(act, lambda e: e.activation(out=hl[:, 0, :], in_=tq[:], func=AF.Copy), reads=[TQ], writes=[HL])
                    plan.op(dve, lambda e: e.tensor_tensor(out=hl[:, 1, :], in0=tq[:], in1=hl[:, 0, :], op=ALU.subtract),
                            reads=[TQ, HL], writes=[HL])
                    plan.group(pe, [lambda e: e.matmul(ps[:, b3, :], lhsT=ropeT_bf, rhs=hl[:, 0, :], start=True, stop=False),
                                    lambda e: e.matmul(ps[:, b3, :], lhsT=ropeT_bf, rhs=hl[:, 1, :], start=False, stop=True)],
                               reads=[HL], writes=[PB[b3]])
                    plan.op(act, lambda e: e.activation(out=rs[:], in_=ps[:, b2, :], func=AF.Sqrt, bias=vcol(EPSC),
                                                        scale=1.0 / 128.0), reads=[PB[b2]], writes=[RS])
                    plan.op(dve, lambda e: e.reciprocal(out=rs[:], in_=rs[:]), reads=[RS], writes=[RS])
                    plan.op(dve, lambda e: e.tensor_tensor(out=tq[:], in0=tq[:], in1=rtab[:, 0, :], op=ALU.mult),
                            reads=[TQ, RTAB], writes=[TQ])
                    plan.op(dve, lambda e: e.tensor_tensor(out=tb_[:], in0=ps[:, b3, :], in1=rtab[:, 1, :], op=ALU.mult),
                            reads=[PB[b3], RTAB], writes=[TBB])
                    plan.op(dve, lambda e: e.tensor_tensor(out=tq[:], in0=tq[:], in1=tb_[:], op=ALU.add),
                            reads=[TQ, TBB], writes=[TQ])
                    q = rST.next()
                    plan.op(dve, lambda e: e.tensor_tensor(out=st16[:, q, :], in0=tq[:], in1=rs[:], op=ALU.mult),
                            reads=[TQ, RS], writes=[ST16[q]])
                    plan.dma(sp, lambda e: e.dma_start(out=dst_ap, in_=st16[:, q, :]), reads=[ST16[q]], writes=[dst_buf])

                def proj_tm(W, col0, ncols, handler):
                    s = load_w(W, col0, ncols, DC)
                    for tt in range(TB // 128):
                        bank = ringA.next()
                        plan.group(pe, [lambda e, c=c, tt=tt, s=s, bank=bank: e.matmul(
                            ps[:, bank, 0:ncols], lhsT=uT[:, c, tt * 128:(tt + 1) * 128], rhs=wgu[:, s, c, 0:ncols],
                            start=(c == 0), stop=(c == DC - 1)) for c in range(DC)],
                            reads=[WS[s]] + UT, writes=[PB[bank]])
                        handler(tt, bank)

                QSP, KSRC, VSRC, FSRC = bufs["QSP"], bufs["KSRC"], bufs["VSRC"], bufs["FSRC"]

                if which == "A":
                    def blockA(b, t0):
                        XH = min(1024, D)
                        for tt in range(TB // 128):
                            for hf in range(D // XH):
                                s = rXIN.next()
                                plan.dma(sp, lambda e, s=s, tt=tt, hf=hf: e.dma_start(
                                    out=xin[:, s, 0:XH], in_=x[t0 + tt * 128:t0 + (tt + 1) * 128, hf * XH:(hf + 1) * XH]),
                                    writes=[XIN[s]])
                                for q in range(XH // 512):
                                    bank = ringB.next()
                                    plan.group(pe, [lambda e, j=j, s=s, q=q, bank=bank: e.transpose(
                                        out=ps[:, bank, j * 128:(j + 1) * 128],
                                        in_=xin[:, s, q * 512 + j * 128:q * 512 + (j + 1) * 128], identity=ident[:])
                                        for j in range(4)], reads=[XIN[s]], writes=[PB[bank]])
                                    c0 = hf * (XH // 128) + q * 4
                                    evac_copy(T32[:, c0:c0 + 4, tt * 128:(tt + 1) * 128],
                                              ps[:, bank, :].rearrange("p (j t) -> p j t", j=4),
                                              [PB[bank]], T32B[c0:c0 + 4])
                        if STOP == "A0":
                            return
                        plan.dma(sp, lambda e: e.dma_start(out=rtab[:], in_=ropetab_d[:, :, t0:t0 + TB]), writes=[RTAB])
                        ffn("ffn1", 0, 1, b, t0)
                        if STOP == "A1":
                            return
                        prenorm(2)
                        spill_T32(b, t0)
                        col = 0
                        while col < QW + KVW:
                            if col < QW:
                                ncols = min(512, QW - col)
                            else:
                                ncols = min(512, QW + KVW - col)

                            def handler(j, bank, col=col):
                                hcol = col + j * 128
                                if hcol < QW:
                                    hq = hcol // 128
                                    head_post(bank, QG, q_sp[hq * 128:(hq + 1) * 128, t0:t0 + TB], QSP)
                                else:
                                    hk = (hcol - QW) // 128
                                    head_post(bank, KG, kT_src[hk * 128:(hk + 1) * 128, t0:t0 + TB], KSRC)
                            proj_fm(w_in, col, ncols, DC, lambda c: uT[:, c, :], UT, handler, ringA)
                            col += ncols
                        if STOP == "A2":
                            return
                        vview = v_src.rearrange("(h p) (k d) -> p h k d", p=128, d=128)

                        def vhandler(tt, bank):
                            q = rST.next()
                            evac_copy(st16[:, q, 0:KVW], ps[:, bank, 0:KVW], [PB[bank]], [ST16[q]])
                            tbl = b * (TB // 128) + tt
                            plan.dma(sp, lambda e: e.dma_start(out=vview[:, :, tbl, :],
                                                               in_=st16[:, q, 0:KVW].rearrange("p (h d) -> p h d", d=128)),
                                     reads=[ST16[q]], writes=[VSRC])
                        proj_tm(w_in, QW + KVW, KVW, vhandler)
                        if STOP == "A3":
                            return
                        fview = f_src.rearrange("(g t) (p c) -> p g t c", g=8, p=128)
                        for cg in range(2):
                            def fhandler(tt, bank, cg=cg):
                                q = rST.next()
                                evac_copy(st16[:, q, :], ps[:, bank, :], [PB[bank]], [ST16[q]])
                                tbl = b * (TB // 128) + tt
                                plan.dma(sp, lambda e: e.dma_start(out=fview[:, cg * 4:(cg + 1) * 4, tbl, :],
                                                                   in_=st16[:, q, :].rearrange("p (g c) -> p g c", c=128)),
                                         reads=[ST16[q]], writes=[FSRC])
                            proj_tm(w_in, QW + 2 * KVW + cg * 512, 512, fhandler)
                    for b in range(NB):
                        blockA(b, b * TB)
                else:
                    G0 = QW + 2 * KVW + FW
                    MRG0, AT0, MT0 = 0, DC, DC + NH
                    def gate_chunk(cg, j, sA, sB, sO, sF):
                        dc = cg * 4 + j
                        bA, bB, bO, bF = ringAll.next(), ringAll.next(), ringAll.next(), ringAll.next()
                        js = slice(j * 128, (j + 1) * 128)
                        plan.group(pe, [lambda e, c=c: e.matmul(ps[:, bA, :], lhsT=wgu[:, sA, c, js], rhs=uT[:, c, :],
                                                                start=(c == 0), stop=(c == DC - 1)) for c in range(DC)],
                                   reads=[WS[sA]] + UT, writes=[PB[bA]])
                        plan.group(pe, [lambda e, c=c: e.matmul(ps[:, bB, :], lhsT=wgu[:, sB, c, js], rhs=uT[:, c, :],
                                                                start=(c == 0), stop=(c == DC - 1)) for c in range(DC)],
                                   reads=[WS[sB]] + UT, writes=[PB[bB]])
                        plan.group(pe, [lambda e, c=c: e.matmul(ps[:, bO, :], lhsT=wgu[:, sO, c, js], rhs=actT[:, AT0 + c, :],
                                                                start=(c == 0), stop=(c == NH - 1)) for c in range(NH)],
                                   reads=[WS[sO]] + ACTB[AT0:AT0 + NH], writes=[PB[bO]])
                        plan.group(pe, [lambda e, c=c: e.matmul(ps[:, bF, :], lhsT=wgu[:, sF, c, js], rhs=actT[:, MT0 + c, :],
                                                                start=(c == 0), stop=(c == 7)) for c in range(8)],
                                   reads=[WS[sF]] + ACTB[MT0:MT0 + 8], writes=[PB[bF]])
                        plan.op(act, lambda e: e.activation(out=tq[:], in_=ps[:, bA, :], func=AF.Sigmoid,
                                                            bias=vcol(6 * DC + dc), scale=1.0),
                                reads=[PB[bA]], writes=[TQ])
                        plan.op(act, lambda e: e.activation(out=tb_[:], in_=ps[:, bB, :], func=AF.Sigmoid,
                                                            bias=vcol(7 * DC + dc), scale=1.0),
                                reads=[PB[bB]], writes=[TBB])
                        plan.op(dve, lambda e: e.tensor_tensor(out=tq[:], in0=tq[:], in1=ps[:, bO, :], op=ALU.mult),
                                reads=[TQ, PB[bO]], writes=[TQ])
                        plan.op(dve, lambda e: e.tensor_tensor(out=tb_[:], in0=tb_[:], in1=ps[:, bF, :], op=ALU.mult),
                                reads=[TBB, PB[bF]], writes=[TBB])
                        plan.op(dve, lambda e: e.tensor_tensor(out=actT[:, MRG0 + dc, :], in0=tq[:], in1=tb_[:], op=ALU.add),
                                reads=[TQ, TBB], writes=[ACTB[MRG0 + dc]])

                    def blockC(b, t0):
                        plan.dma(sp, lambda e: e.dma_start(out=T32[:], in_=hsp_view(t0)), reads=[HSP[b]], writes=T32B)
                        prenorm(2)
                        plan.dma(sp, lambda e: e.dma_start(
                            out=actT[:, AT0:AT0 + NH, :], in_=q_sp.rearrange("(h d) t -> d h t", d=128)[:, :, t0:t0 + TB]),
                            reads=[QSP], writes=ACTB[AT0:AT0 + NH])

                        def mload(e):
                            pid = get_pid(e, "C")
                            return e.dma_start(out=actT[:, MT0:MT0 + 8, :],
                                               in_=mix_all.rearrange("(g m) s -> m g s", m=128)[:, :, bass.ds(pid * TPC + t0, TB)])
                        plan.dma(sp, mload, reads=[bufs["MIXALL"]], writes=ACTB[MT0:MT0 + 8])
                        for cg in range(D // 512):
                            sA = load_w(w_in, G0 + cg * 512, 512, DC)
                            sB = load_w(w_in, G0 + D + cg * 512, 512, DC)
                            sO = load_w(w_attn_o, cg * 512, 512, NH)
                            sF = load_w(w_fourier, cg * 512, 512, 8)
                            for j in range(4):
                                gate_chunk(cg, j, sA, sB, sO, sF)
                        for cg in range(D // 512):
                            def ohandler(j, bank, cg=cg):
                                c = cg * 4 + j
                                evac_copy(T32[:, c, :], ps[:, bank, :], [PB[bank]], [T32B[c]])
                            proj_fm(w_out, cg * 512, 512, DC, lambda c: actT[:, MRG0 + c, :], ACTB[MRG0:MRG0 + DC],
                                    ohandler, ringAll)
                        postnorm_residual(3, 1.0, b, t0)
                        ffn("ffn2", 4, 5, b, t0)
                        for tt in range(TB // 128):
                            for cq in range(DC // 4):
                                bank = ringAll.next()
                                plan.group(pe, [lambda e, j=j, cq=cq, tt=tt, bank=bank: e.transpose(
                                    out=ps[:, bank, j * 128:(j + 1) * 128], in_=T32[:, cq * 4 + j, tt * 128:(tt + 1) * 128],
                                    identity=ident[:]) for j in range(4)],
                                    reads=T32B[cq * 4:cq * 4 + 4], writes=[PB[bank]])
                                s = rYST.next()
                                evac_copy(yst[:, s, :], ps[:, bank, :], [PB[bank]], [YST[s]])
                                plan.dma(sp, lambda e, s=s, tt=tt, cq=cq: e.dma_start(
                                    out=y[t0 + tt * 128:t0 + (tt + 1) * 128, cq * 512:(cq + 1) * 512], in_=yst[:, s, :]),
                                    reads=[YST[s]], writes=[bufs["Y"]])
                    for b in range(NB):
                        blockC(b, b * TB)
                plan.barrier()
                run_block(which)

        bufs = {k: Buf(k) for k in ["QSP", "KSRC", "VSRC", "FSRC", "KALL", "VALL", "FALL", "MIXSRC", "MIXALL", "Y"]}

        token_phase("A")

        def exchange_phase():
            plan.coll(pool, lambda e: e.collective_compute("AllGather", ALU.bypass, replica_groups=RG, ins=[f_src], outs=[f_all]),
                      cc_sems[2], reads=[bufs["FSRC"]], writes=[bufs["FALL"]])
            plan.coll(pool, lambda e: e.collective_compute("AllGather", ALU.bypass, replica_groups=RG, ins=[kT_src], outs=[kT_all]),
                      cc_sems[0], reads=[bufs["KSRC"]], writes=[bufs["KALL"]])
            plan.coll(pool, lambda e: e.collective_compute("AllGather", ALU.bypass, replica_groups=RG, ins=[v_src], outs=[v_all]),
                      cc_sems[1], reads=[bufs["VSRC"]], writes=[bufs["VALL"]])

        def fourier_phase():
            with ExitStack() as f_es:
                def sbf(name, shape, dt):
                    return f_es.enter_context(nc.sbuf_tensor(name, list(shape), dt))
                Fsb = sbf("Fsb", [128, 128, 128], BF16)
                Yr = sbf("Yr", [128, 128, S1], BF16)
                Yi = sbf("Yi", [128, 128, S1], BF16)
                HS = 2 if S1 > 64 else 1
                S1h = S1 // HS
                GrT = sbf("GrT", [128, HS, 128, S1h], BF16)
                GiT = sbf("GiT", [128, HS, 128, S1h], BF16)
                tw = sbf("tw_sb", [128, 2, CPB, S1], F32)
                tmp = sbf("ftmp", [128, 4, CPB, S1], F32)
                mst = sbf("mst", [128, 2, 512], BF16)
                FSB, YR, YI, GRT, GIT, TW = Buf("fsb"), Buf("yr"), Buf("yi"), Buf("grt"), Buf("git"), Buf("tw")
                TMP = [Buf("ftmp%d" % i) for i in range(4)]
                MST = [Buf("mst%d" % i) for i in range(2)]
                rMST = Ring(range(2))
                ringAll = Ring(range(8))
                stopF = {"F0": 1, "F1": 2, "F2": 3}.get(STOP, 9)
                if stopF >= 1:
                    plan.dma(sp, lambda e: e.dma_start(out=tw[:], in_=tw_d), writes=[TW])
                    fv = f_all.rearrange("(r g t) x -> r g t x", r=NCORES, g=8)
                    for r in range(NCORES):
                        def fload(e, r=r):
                            pid = get_pid(e, "F")
                            return e.dma_start(out=Fsb[r * TBL:(r + 1) * TBL].rearrange("p a b -> p (a b)"),
                                               in_=fv[r, bass.ds(pid, 1), :, :])
                        plan.dma(sp, fload, reads=[bufs["FALL"]], writes=[FSB])
                if stopF >= 2:
                    for cb in range(128 // CPB):
                        bank = ringAll.next()
                        plan.group(pe, [lambda e, cl=cl, cb=cb, bank=bank: e.matmul(
                            ps[:, bank, cl * 2 * S1:(cl + 1) * 2 * S1], lhsT=Fsb[0:S1, :, cb * CPB + cl], rhs=D1,
                            start=True, stop=True) for cl in range(CPB)], reads=[FSB], writes=[PB[bank]])
                        psv = ps[:, bank, 0:CPB * 2 * S1].rearrange("p (l r k) -> p l r k", l=CPB, r=2)
                        yr_ps, yi_ps = psv[:, :, 0, :], psv[:, :, 1, :]
                        c0 = cb * CPB
                        plan.op(dve, lambda e, yr_ps=yr_ps: e.tensor_tensor(out=tmp[:, 0], in0=yr_ps, in1=tw[:, 0], op=ALU.mult),
                                reads=[PB[bank], TW], writes=[TMP[0]])
                        plan.op(dve, lambda e, yi_ps=yi_ps: e.tensor_tensor(out=tmp[:, 1], in0=yi_ps, in1=tw[:, 1], op=ALU.mult),
                                reads=[PB[bank], TW], writes=[TMP[1]])
                        plan.op(dve, lambda e, c0=c0: e.tensor_tensor(out=Yr[:, c0:c0 + CPB, :], in0=tmp[:, 0], in1=tmp[:, 1], op=ALU.add),
                                reads=[TMP[0], TMP[1]], writes=[YR])
                        plan.op(dve, lambda e, yi_ps=yi_ps: e.tensor_tensor(out=tmp[:, 2], in0=yi_ps, in1=tw[:, 0], op=ALU.mult),
                                reads=[PB[bank], TW], writes=[TMP[2]])
                        plan.op(dve, lambda e, yr_ps=yr_ps: e.tensor_tensor(out=tmp[:, 3], in0=yr_ps, in1=tw[:, 1], op=ALU.mult),
                                reads=[PB[bank], TW], writes=[TMP[3]])
                        plan.op(dve, lambda e, c0=c0: e.tensor_tensor(out=Yi[:, c0:c0 + CPB, :], in0=tmp[:, 2], in1=tmp[:, 3],
                                                                       op=ALU.subtract),
                                reads=[TMP[2], TMP[3]], writes=[YI])
                plan.barrier()
                if stopF >= 3:
                    for k1p in range(S1 // 2):
                        bank = ringAll.next()
                        fns = []
                        for kk in range(2):
                            k1 = 2 * k1p + kk
                            fns.append(lambda e, kk=kk, k1=k1, bank=bank: e.matmul(ps[:, bank, kk * 256:(kk + 1) * 256], lhsT=Yr[:, :, k1],
                                                                                  rhs=D2a, start=True, stop=False))
                            fns.append(lambda e, kk=kk, k1=k1, bank=bank: e.matmul(ps[:, bank, kk * 256:(kk + 1) * 256], lhsT=Yi[:, :, k1],
                                                                                  rhs=D2b, start=False, stop=True))
                        plan.group(pe, fns, reads=[YR, YI], writes=[PB[bank]])
                        for kk in range(2):
                            k1 = 2 * k1p + kk
                            plan.op(act, lambda e, kk=kk, k1=k1, bank=bank: e.activation(
                                out=GrT[:, k1 // S1h, :, k1 % S1h], in_=ps[:, bank, kk * 256:kk * 256 + 128], func=AF.Copy),
                                reads=[PB[bank]], writes=[GRT])
                            plan.op(act, lambda e, kk=kk, k1=k1, bank=bank: e.activation(
                                    out=GiT[:, k1 // S1h, :, k1 % S1h], in_=ps[:, bank, kk * 256 + 128:kk * 256 + 256], func=AF.Copy),
                                reads=[PB[bank]], writes=[GIT])
                plan.barrier()
                if stopF >= 4:
                    NK2 = 512 // S1
                    fscale = float(1.0 / np.sqrt(float(SEQ) * 128.0))
                    for tch in range(SEQ // 512):
                        bank = ringAll.next()
                        fns = []
                        for k2l in range(NK2):
                            k2 = tch * NK2 + k2l
                            for hf in range(HS):
                                c0_ = k2l * S1 + hf * S1h
                                fns.append(lambda e, k2=k2, hf=hf, c0_=c0_, bank=bank: e.matmul(
                                    ps[:, bank, c0_:c0_ + S1h], lhsT=Cc, rhs=GrT[:, hf, k2, :], start=True, stop=False))
                                fns.append(lambda e, k2=k2, hf=hf, c0_=c0_, bank=bank: e.matmul(
                                    ps[:, bank, c0_:c0_ + S1h], lhsT=Sc, rhs=GiT[:, hf, k2, :], start=False, stop=True))
                        plan.group(pe, fns, reads=[GRT, GIT], writes=[PB[bank]])
                        s = rMST.next()
                        plan.op(act, lambda e, s=s, bank=bank: e.activation(out=mst[:, s, :], in_=ps[:, bank, :], func=AF.Copy, scale=fscale),
                                reads=[PB[bank]], writes=[MST[s]])
                        plan.dma(sp, lambda e, s=s, tch=tch: e.dma_start(out=mix_src[:, tch * 512:(tch + 1) * 512], in_=mst[:, s, :]),
                                 reads=[MST[s]], writes=[bufs["MIXSRC"]])
                    plan.coll(pool, lambda e: e.collective_compute("AllGather", ALU.bypass, replica_groups=RG, ins=[mix_src], outs=[mix_all]),
                              cc_sems[3], reads=[bufs["MIXSRC"]], writes=[bufs["MIXALL"]])
                plan.barrier()
                run_block("F")

        def attention_phase():
            with ExitStack() as a_es:
                def sba(name, shape, dt):
                    return a_es.enter_context(nc.sbuf_tensor(name, list(shape), dt))
                QT = sba("QT", [128, NH, TPC], BF16)
                kT = sba("kT", [128, SEQ], BF16)
                vS = sba("vS", [128, KC, 128], BF16)
                pT = sba("pT", [128, 4, 512], BF16)
                rden = sba("rden", [128, 2, 512], F32)
                QTB = [[Buf("qt%d_%d" % (h, qb)) for qb in range(NB)] for h in range(NH)]
                KT, VS = Buf("kt"), Buf("vs")
                PT = [Buf("pt%d" % i) for i in range(4)]
                RDEN = [Buf("rden%d" % i) for i in range(2)]
                rPT, rRD = Ring(range(4)), Ring(range(2))
                ringS, ringO, ringD = Ring([0, 1, 2, 3]), Ring([4, 5]), Ring([6, 7])
                qsv = q_sp.rearrange("(h d) t -> d h t", d=128)
                for h in range(NH):
                    plan.dma(sp, lambda e, h=h: e.dma_start(out=QT[:, h, :], in_=qsv[:, h, :]), reads=[bufs["QSP"]], writes=QTB[h])
                kav = kT_all.rearrange("(r h d) t -> d h r t", r=NCORES, d=128)
                vav = v_all.rearrange("(r h p) (k d) -> p h r k d", r=NCORES, p=128, d=128)
                for kvh in range(NKV):
                    plan.dma(sp, lambda e, kvh=kvh: e.dma_start(out=kT[:].rearrange("d (r t) -> d r t", r=NCORES), in_=kav[:, kvh]),
                             reads=[bufs["KALL"]], writes=[KT])
                    plan.dma(sp, lambda e, kvh=kvh: e.dma_start(out=vS[:].rearrange("p (r k) d -> p r k d", r=NCORES), in_=vav[:, kvh]),
                             reads=[bufs["VALL"]], writes=[VS])
                    for qh in range(4):
                        h = kvh * 4 + qh
                        for qb in range(NB):
                            ob, db = ringO.next(), ringD.next()
                            qr = QT[:, h, qb * TB:(qb + 1) * TB]

                            def S(kc, qr=qr, h=h, qb=qb):
                                sbk = ringS.next()
                                plan.group(pe, [lambda e: e.matmul(ps[:, sbk, :], lhsT=kT[:, kc * 128:(kc + 1) * 128], rhs=qr,
                                                                   start=True, stop=True)],
                                           reads=[KT, QTB[h][qb]], writes=[PB[sbk]])
                                return sbk
                            sbs = {0: S(0)}
                            if KC > 1:
                                sbs[1] = S(1)
                            for kc in range(KC):
                                p = rPT.next()
                                sbk = sbs.pop(kc)
                                plan.op(act, lambda e, p=p, sbk=sbk: e.activation(out=pT[:, p, :], in_=ps[:, sbk, :], func=AF.Exp,
                                                                                  scale=SCALE),
                                        reads=[PB[sbk]], writes=[PT[p]])
                                if kc + 2 < KC:
                                    sbs[kc + 2] = S(kc + 2)
                                plan.group(pe, [
                                    lambda e, p=p, kc=kc, ob=ob: e.matmul(ps[:, ob, :], lhsT=vS[:, kc, :], rhs=pT[:, p, :],
                                                                          start=(kc == 0), stop=(kc == KC - 1)),
                                    lambda e, p=p, kc=kc, db=db: e.matmul(ps[:, db, :], lhsT=ones_bf, rhs=pT[:, p, :],
                                                                          start=(kc == 0), stop=(kc == KC - 1))],
                                    reads=[VS, PT[p]], writes=[PB[ob], PB[db]])
                            rd = rRD.next()
                            plan.op(dve, lambda e, rd=rd, db=db: e.reciprocal(out=rden[:, rd, :], in_=ps[:, db, :]),
                                    reads=[PB[db]], writes=[RDEN[rd]])
                            plan.op(dve, lambda e, rd=rd, ob=ob, qr=qr: e.tensor_tensor(out=qr, in0=ps[:, ob, :], in1=rden[:, rd, :],
                                                                                        op=ALU.mult),
                                    reads=[PB[ob], RDEN[rd]], writes=[QTB[h][qb]])
                        plan.dma(sp, lambda e, h=h: e.dma_start(out=qsv[:, h, :], in_=QT[:, h, :]), reads=QTB[h], writes=[bufs["QSP"]])
                plan.barrier()
                run_block("T")

        if STOP[:1] not in ("A",):
            exchange_phase()
            if STOP != "FX":
                fourier_phase()
        if STOP[:1] not in ("A", "F"):
            attention_phase()
        if STOP[:1] not in ("A", "F", "T"):
            token_phase("C")

        plan.barrier(only=["sp"])
        run_block("Z")
    return nc


def _consts(cfg):
    D, SEQ = cfg["D"], cfg["SEQ"]
    DC = D // 128
    TPC = SEQ // NCORES
    S1 = SEQ // 128
    CPB = max(1, 512 // (2 * S1))
    ident = np.eye(128, dtype=np.float32)
    R = np.zeros((128, 128), np.float32)
    for base in (0, 64):
        for i in range(32):
            R[base + i, base + i + 32] = -1.0
            R[base + i + 32, base + i] = 1.0
    ropeT = np.ascontiguousarray(R.T)
    s2 = np.arange(128)[:, None].astype(np.float64)
    k1 = np.arange(S1)[None, :].astype(np.float64)
    th = 2.0 * np.pi * s2 * k1 / SEQ
    tw = np.zeros((128, 2, CPB, S1), np.float32)
    for l in range(CPB):
        tw[:, 0, l, :] = np.cos(th)
        tw[:, 1, l, :] = np.sin(th)
    a = np.arange(S1)[:, None].astype(np.float64) * np.arange(S1)[None, :] * 2.0 * np.pi / S1
    D1 = np.zeros((128, 2 * S1))
    D1[:S1, :S1] = np.cos(a)
    D1[:S1, S1:] = -np.sin(a)
    a2 = np.arange(128)[:, None].astype(np.float64) * np.arange(128)[None, :] * 2.0 * np.pi / 128
    C, S = np.cos(a2), np.sin(a2)
    D2a = np.concatenate([C, -S], 1)
    D2b = np.concatenate([S, C], 1)
    cb = np.concatenate([D1, D2a, D2b, C, S, np.ones((128, 128)), ropeT.astype(np.float64)], 1).astype(ml_dtypes.bfloat16)
    t = np.arange(SEQ)
    row = (t // 64).astype(np.float32)
    colp = (t % 64).astype(np.float32)
    inv = (np.float32(10000.0) ** (-np.arange(32, dtype=np.float32) / np.float32(32))).astype(np.float32)
    ang_r = (row[:, None] * inv[None, :]).astype(np.float32)
    ang_c = (colp[:, None] * inv[None, :]).astype(np.float32)
    cosd = np.zeros((128, SEQ), np.float32)
    sind = np.zeros((128, SEQ), np.float32)
    for d in range(128):
        ang = ang_r if d < 64 else ang_c
        cosd[d] = np.cos(ang[:, d % 32])
        sind[d] = np.sin(ang[:, d % 32])
    ropetabs = []
    for c in range(NCORES):
        rt = np.stack([cosd[:, c * TPC:(c + 1) * TPC], sind[:, c * TPC:(c + 1) * TPC]], 1)
        ropetabs.append(np.ascontiguousarray(rt, dtype=np.float32))
    return ident, ropeT, tw, np.ascontiguousarray(cb), ropetabs


_CACHE = {}


def run(inputs, cfg):
    D, SEQ = cfg["D"], cfg["SEQ"]
    DC = D // 128
    TPC = SEQ // NCORES
    key = (cfg["D"], cfg["FF"], cfg["SEQ"])
    if key not in _CACHE:
        _CACHE[key] = (build_program(cfg), _consts(cfg))
    nc, (ident, ropeT, tw, cb, ropetabs) = _CACHE[key]

    def g(name):
        return np.asarray(inputs[name], dtype=np.float32)

    def col(v):
        return np.ascontiguousarray(v.reshape(-1, 128).T)

    vecs = np.zeros((128, 8 * DC + 4), np.float32)
    for i, n in enumerate(["ffn1_pre_g", "ffn1_post_g", "mix_pre_g", "mix_post_g", "ffn2_pre_g", "ffn2_post_g"]):
        vecs[:, i * DC:(i + 1) * DC] = col(g(n)[0])
    bg = g("b_gate")[0]
    vecs[:, 6 * DC:7 * DC] = col(bg[0])
    vecs[:, 7 * DC:8 * DC] = col(bg[1])
    vecs[:, 8 * DC] = g("q_norm_g")[0]
    vecs[:, 8 * DC + 1] = g("k_norm_g")[0]
    vecs[:, 8 * DC + 2] = EPS
    xs = g("x")[0]
    shared = {"vecs": vecs, "ident": ident, "ropeT": ropeT, "tw": tw, "cb16": cb,
              "w_in": g("w_in")[0], "w_attn_o": g("w_attn_o")[0], "w_fourier": g("w_fourier")[0], "w_out": g("w_out")[0]}
    for f in ("ffn1", "ffn2"):
        for n in ("_w_gate", "_w_up", "_w_down"):
            shared[f + n] = g(f + n)[0]
    in_maps = []
    for c in range(NCORES):
        m = dict(shared)
        m["x"] = np.ascontiguousarray(xs[c * TPC:(c + 1) * TPC])
        m["ropetab"] = ropetabs[c]
        in_maps.append(m)
    res = run_bass_kernel_spmd(nc, in_maps, core_ids=list(range(NCORES)))
    out = np.concatenate([np.asarray(r["y"]) for r in res.results], axis=0)
    return out.reshape(1, SEQ, D).astype(np.float32)


def kernel(**inputs):
    return run(inputs, FULL_CFG)
```
